# Optimizing a Trainium2 kernel written in Bass

```python
import jax
import jax.numpy as jnp
from jax import lax
import numpy as np

D_MODEL = 1024
BATCH = 2
SEQ = 16384
DEPTH = 4

RMS_EPS = 1e-6
SSD_HEADS = 16
SSD_HEAD_DIM = 64
SSD_INNER = SSD_HEADS * SSD_HEAD_DIM
SSD_STATE = 128
SSD_GROUPS = 4
SSD_CONV = 4
SSD_CHUNK = 128
SSD_CONV_DIM = SSD_INNER + 2 * SSD_GROUPS * SSD_STATE
GLA_HEADS = 4
GLA_DK = 64
GLA_DV = 128
GLA_KEY = GLA_HEADS * GLA_DK
GLA_VAL = GLA_HEADS * GLA_DV
GLA_RANK = 16
GLA_TAU = 16.0
GLA_CHUNK = 16
ATT_PATTERNS = ((128, 1), (512, 4), (2048, 16))
ATT_GROUPS = 3
ATT_HEADS_PER_GROUP = 4
ATT_HEADS = ATT_GROUPS * ATT_HEADS_PER_GROUP
ATT_HEAD_DIM = 128
ATT_OUT = ATT_HEADS_PER_GROUP * ATT_HEAD_DIM
ATT_BLOCK = 128
ROPE_THETA = 10000.0
N_BRANCH = 3
BRANCH_WIDTHS = (SSD_INNER, GLA_VAL, ATT_OUT)
BRANCH_TOTAL = SSD_INNER + GLA_VAL + ATT_OUT
IN_WIDTHS = (SSD_INNER, SSD_CONV_DIM, SSD_HEADS, GLA_KEY, GLA_KEY, GLA_VAL, GLA_RANK, GLA_VAL,
             3 * ATT_HEADS * ATT_HEAD_DIM, N_BRANCH * D_MODEL)
IN_COLS = sum(IN_WIDTHS)
FFN_HIDDEN = 2816
FFN_CONV = 3

kernel_name = 'hybrid_ssd_gla_dilated_convffn'


def _split_points(widths):
    return [int(v) for v in np.cumsum(widths)[:-1]]


def rmsnorm(x, w):
    xf = x.astype(jnp.float32)
    y = xf * lax.rsqrt(jnp.mean(xf * xf, axis=-1, keepdims=True) + RMS_EPS)
    return (y * w.astype(jnp.float32)).astype(x.dtype)


def group_rmsnorm(x, w, groups):
    shp = x.shape
    xg = x.reshape(shp[:-1] + (groups, shp[-1] // groups))
    xg = xg * lax.rsqrt(jnp.mean(xg * xg, axis=-1, keepdims=True) + RMS_EPS)
    return xg.reshape(shp) * w.astype(jnp.float32)


def causal_dwconv(x, w, b):
    width, ch = w.shape
    y = lax.conv_general_dilated(x, w[:, None, :].astype(x.dtype), (1,), [(width - 1, 0)],
                                 dimension_numbers=('NWC', 'WIO', 'NWC'), feature_group_count=ch)
    return y + b.astype(x.dtype)


def rope(x, pos):
    half = x.shape[-1] // 2
    inv = ROPE_THETA ** (-jnp.arange(half, dtype=jnp.float32) / half)
    ang = pos[:, None] * inv[None, :]
    cos = jnp.cos(ang)[:, None, :]
    sin = jnp.sin(ang)[:, None, :]
    x1, x2 = x[..., :half], x[..., half:]
    return jnp.concatenate([x1 * cos - x2 * sin, x2 * cos + x1 * sin], axis=-1)


def chunk_prefix_states(chunk_decay, chunk_states):
    def step(h, inp):
        dcy, st = inp
        return dcy * h + st, h
    init = jnp.zeros_like(chunk_states[:, 0])
    _, prev = lax.scan(step, init, (jnp.moveaxis(chunk_decay, 1, 0), jnp.moveaxis(chunk_states, 1, 0)))
    return jnp.moveaxis(prev, 0, 1)


def ssd_chunked_scan(x, a, b, c):
    bsz, seq, nh, p = x.shape
    g, n = b.shape[2], b.shape[3]
    hg = nh // g
    L = SSD_CHUNK
    nc = seq // L
    x = x.reshape(bsz, nc, L, g, hg, p)
    a = a.reshape(bsz, nc, L, g, hg)
    b = b.reshape(bsz, nc, L, g, n)
    c = c.reshape(bsz, nc, L, g, n)
    a_cum = jnp.cumsum(a, axis=2)
    causal = jnp.tril(jnp.ones((L, L), dtype=bool))
    seg = a_cum[:, :, :, None] - a_cum[:, :, None, :]
    decay = jnp.exp(jnp.where(causal[:, :, None, None], seg, -jnp.inf))
    cb = jnp.einsum('bctgn,bcsgn->bctsg', c, b)
    y_diag = jnp.einsum('bctsg,bctsgh,bcsghp->bctghp', cb, decay, x)
    to_end = jnp.exp(a_cum[:, :, -1:] - a_cum)
    states = jnp.einsum('bcsgn,bcsgh,bcsghp->bcghpn', b, to_end, x)
    prev = chunk_prefix_states(jnp.exp(a_cum[:, :, -1])[..., None, None], states)
    y_off = jnp.einsum('bctgn,bcghpn,bctgh->bctghp', c, prev, jnp.exp(a_cum))
    return (y_diag + y_off).reshape(bsz, seq, nh, p)


def ssd_mixer(z, xbc, dt_raw, conv_w, conv_b, dt_bias, a_log, d_skip, norm_w):
    bsz, seq, _ = z.shape
    xbc = jax.nn.silu(causal_dwconv(xbc, conv_w, conv_b)).astype(jnp.float32)
    xs, bm, cm = jnp.split(xbc, [SSD_INNER, SSD_INNER + SSD_GROUPS * SSD_STATE], axis=-1)
    xs = xs.reshape(bsz, seq, SSD_HEADS, SSD_HEAD_DIM)
    bm = bm.reshape(bsz, seq, SSD_GROUPS, SSD_STATE)
    cm = cm.reshape(bsz, seq, SSD_GROUPS, SSD_STATE)
    dt = jax.nn.softplus(dt_raw.astype(jnp.float32) + dt_bias.astype(jnp.float32))
    a = -jnp.exp(a_log.astype(jnp.float32))
    y = ssd_chunked_scan(xs * dt[..., None], dt * a, bm, cm)
    y = y + d_skip.astype(jnp.float32)[:, None] * xs
    y = y.reshape(bsz, seq, SSD_INNER) * jax.nn.silu(z.astype(jnp.float32))
    y = group_rmsnorm(y, norm_w, SSD_GROUPS)
    return y.astype(z.dtype)


def gla_chunked(q, k, v, log_a):
    bsz, seq, nh, dk = q.shape
    dv = v.shape[-1]
    C = GLA_CHUNK
    nc = seq // C
    q = q.reshape(bsz, nc, C, nh, dk)
    k = k.reshape(bsz, nc, C, nh, dk)
    v = v.reshape(bsz, nc, C, nh, dv)
    bcum = jnp.cumsum(log_a.reshape(bsz, nc, C, nh, dk), axis=2)
    causal = jnp.tril(jnp.ones((C, C), dtype=bool))
    diff = bcum[:, :, :, None] - bcum[:, :, None, :]
    decay = jnp.exp(jnp.where(causal[:, :, None, None], diff, -jnp.inf))
    scores = jnp.einsum('bcthd,bcshd,bctshd->bchts', q, k, decay)
    o_intra = jnp.einsum('bchts,bcshe->bcthe', scores, v)
    k_end = k * jnp.exp(bcum[:, :, -1:] - bcum)
    states = jnp.einsum('bcshd,bcshe->bchde', k_end, v)
    prev = chunk_prefix_states(jnp.exp(bcum[:, :, -1])[..., None], states)
    o_inter = jnp.einsum('bcthd,bchde->bcthe', q * jnp.exp(bcum), prev)
    return (o_intra + o_inter).reshape(bsz, seq, nh, dv)


def gla_mixer(q, k, v, gate_lr, r, gate_w, gate_b, norm_w):
    bsz, seq, _ = q.shape
    f32 = jnp.float32
    qh = q.astype(f32).reshape(bsz, seq, GLA_HEADS, GLA_DK) * (GLA_DK ** -0.5)
    kh = k.astype(f32).reshape(bsz, seq, GLA_HEADS, GLA_DK)
    vh = v.astype(f32).reshape(bsz, seq, GLA_HEADS, GLA_DV)
    pre = gate_lr.astype(f32) @ gate_w.astype(f32) + gate_b.astype(f32)
    log_a = (jax.nn.log_sigmoid(pre) / GLA_TAU).reshape(bsz, seq, GLA_HEADS, GLA_DK)
    o = gla_chunked(qh, kh, vh, log_a)
    o = rmsnorm(o, norm_w) * jax.nn.silu(r.astype(f32).reshape(bsz, seq, GLA_HEADS, GLA_DV))
    return o.reshape(bsz, seq, GLA_VAL).astype(q.dtype)


def dilated_window_attention(q, k, v, window, dilation):
    bsz, seq, nh, dh = q.shape
    w_sub = window // dilation
    span = dilation * ATT_BLOCK
    seq_pad = -(-seq // span) * span
    L = seq_pad // dilation
    nb = L // ATT_BLOCK

    def to_sub(t):
        t = jnp.pad(t, ((0, 0), (0, seq_pad - seq), (0, 0), (0, 0)))
        t = t.reshape(bsz, L, dilation, nh, dh).transpose(0, 2, 1, 3, 4)
        return t.reshape(bsz * dilation, nb, ATT_BLOCK, nh, dh)

    def with_prev(t):
        prev = jnp.pad(t[:, :-1], ((0, 0), (1, 0), (0, 0), (0, 0), (0, 0)))
        return jnp.concatenate([prev, t], axis=2)

    qs = to_sub(q)
    ks = with_prev(to_sub(k))
    vs = with_prev(to_sub(v))
    s = jnp.einsum('bnqhd,bnkhd->bnhqk', qs, ks) * (dh ** -0.5)
    qi = jnp.arange(ATT_BLOCK)[:, None]
    ki = jnp.arange(2 * ATT_BLOCK)[None, :]
    dist = qi + ATT_BLOCK - ki
    blk = jnp.arange(nb)[:, None, None]
    valid = (dist >= 0) & (dist <= w_sub) & ((blk > 0) | (ki >= ATT_BLOCK))
    s = jnp.where(valid[None, :, None], s, -jnp.inf)
    lse = jax.nn.logsumexp(s, axis=-1)
    p = jnp.exp(s - lse[..., None])
    o = jnp.einsum('bnhqk,bnkhd->bnqhd', p, vs)
    o = o.reshape(bsz, dilation, L, nh, dh).transpose(0, 2, 1, 3, 4).reshape(bsz, seq_pad, nh, dh)[:, :seq]
    lse = lse.transpose(0, 1, 3, 2).reshape(bsz, dilation, L, nh).transpose(0, 2, 1, 3)
    lse = lse.reshape(bsz, seq_pad, nh)[:, :seq]
    return o, lse


def dilated_attention_mixer(qkv):
    bsz, seq, _ = qkv.shape
    t = qkv.astype(jnp.float32).reshape(bsz, seq, 3, ATT_HEADS, ATT_HEAD_DIM)
    pos = jnp.arange(seq, dtype=jnp.float32)
    shp = (bsz, seq, ATT_GROUPS, ATT_HEADS_PER_GROUP, ATT_HEAD_DIM)
    q = rope(t[:, :, 0], pos).reshape(shp)
    k = rope(t[:, :, 1], pos).reshape(shp)
    v = t[:, :, 2].reshape(shp)
    outs = []
    lses = []
    for g, (window, dilation) in enumerate(ATT_PATTERNS):
        o_g, lse_g = dilated_window_attention(q[:, :, g], k[:, :, g], v[:, :, g], window, dilation)
        outs.append(o_g)
        lses.append(lse_g)
    wts = jax.nn.softmax(jnp.stack(lses, axis=0), axis=0)
    o = jnp.sum(wts[..., None] * jnp.stack(outs, axis=0), axis=0)
    return o.reshape(bsz, seq, ATT_OUT).astype(qkv.dtype)


def hybrid_mixer(hn, w_in, ssd_conv_w, ssd_conv_b, ssd_dt_bias, ssd_a_log, ssd_d, ssd_norm_w,
                 gla_gate_w, gla_gate_b, gla_norm_w, w_branch, w_out):
    bsz, seq, _ = hn.shape
    proj = hn @ w_in.astype(hn.dtype)
    (z, xbc, dt_raw, gq, gk, gv, g_lr, g_r, qkv, gates) = jnp.split(proj, _split_points(IN_WIDTHS), axis=-1)
    y_ssd = ssd_mixer(z, xbc, dt_raw, ssd_conv_w, ssd_conv_b, ssd_dt_bias, ssd_a_log, ssd_d, ssd_norm_w)
    y_gla = gla_mixer(gq, gk, gv, g_lr, g_r, gla_gate_w, gla_gate_b, gla_norm_w)
    y_att = dilated_attention_mixer(qkv)
    gates = jax.nn.sigmoid(gates.astype(jnp.float32)).reshape(bsz, seq, N_BRANCH, D_MODEL).astype(hn.dtype)
    wb_ssd, wb_gla, wb_att = jnp.split(w_branch.astype(hn.dtype), _split_points(BRANCH_WIDTHS), axis=0)
    merged = (gates[:, :, 0] * (y_ssd @ wb_ssd)
              + gates[:, :, 1] * (y_gla @ wb_gla)
              + gates[:, :, 2] * (y_att @ wb_att))
    return merged @ w_out.astype(hn.dtype)


def conv_ffn(hn, up, conv_w, conv_b, down):
    gate, val = jnp.split(hn @ up.astype(hn.dtype), 2, axis=-1)
    gate = causal_dwconv(gate, conv_w, conv_b)
    return (jax.nn.silu(gate) * val) @ down.astype(hn.dtype)


def setup_inputs(seed: int = 0) -> dict:
    key = jax.random.key(seed)
    ks = jax.random.split(key, 24)
    f32 = jnp.float32

    def nrm(k, shape, scale):
        return jax.random.normal(k, shape, f32) * scale

    x = jax.random.normal(ks[0], (BATCH, SEQ, D_MODEL), f32)
    norm1_w = 1.0 + nrm(ks[1], (DEPTH, D_MODEL), 0.05)
    w_in = nrm(ks[2], (DEPTH, D_MODEL, IN_COLS), D_MODEL ** -0.5)
    ssd_conv_w = nrm(ks[3], (DEPTH, SSD_CONV, SSD_CONV_DIM), SSD_CONV ** -0.5)
    ssd_conv_b = nrm(ks[4], (DEPTH, SSD_CONV_DIM), 0.02)
    u = jax.random.uniform(ks[5], (DEPTH, SSD_HEADS), f32)
    dt0 = jnp.exp(u * (np.log(0.1) - np.log(0.001)) + np.log(0.001))
    ssd_dt_bias = dt0 + jnp.log(-jnp.expm1(-dt0))
    ssd_a_log = jnp.log(jax.random.uniform(ks[6], (DEPTH, SSD_HEADS), f32, 1.0, 16.0))
    ssd_d = 1.0 + nrm(ks[7], (DEPTH, SSD_HEADS), 0.1)
    ssd_norm_w = 1.0 + nrm(ks[8], (DEPTH, SSD_INNER), 0.05)
    gla_gate_w = nrm(ks[9], (DEPTH, GLA_RANK, GLA_KEY), GLA_RANK ** -0.5)
    gla_gate_b = nrm(ks[10], (DEPTH, GLA_KEY), 0.1)
    gla_norm_w = 1.0 + nrm(ks[11], (DEPTH, GLA_DV), 0.05)
    w_branch = jnp.concatenate([
        nrm(ks[12], (DEPTH, SSD_INNER, D_MODEL), SSD_INNER ** -0.5),
        nrm(ks[13], (DEPTH, GLA_VAL, D_MODEL), GLA_VAL ** -0.5),
        nrm(ks[14], (DEPTH, ATT_OUT, D_MODEL), ATT_OUT ** -0.5)], axis=1)
    w_out = nrm(ks[15], (DEPTH, D_MODEL, D_MODEL), D_MODEL ** -0.5)
    norm2_w = 1.0 + nrm(ks[16], (DEPTH, D_MODEL), 0.05)
    ffn_up = nrm(ks[17], (DEPTH, D_MODEL, 2 * FFN_HIDDEN), D_MODEL ** -0.5)
    ffn_conv_w = nrm(ks[18], (DEPTH, FFN_CONV, FFN_HIDDEN), FFN_CONV ** -0.5)
    ffn_conv_b = nrm(ks[19], (DEPTH, FFN_HIDDEN), 0.02)
    ffn_down = nrm(ks[20], (DEPTH, FFN_HIDDEN, D_MODEL), FFN_HIDDEN ** -0.5)
    final_norm_w = 1.0 + nrm(ks[21], (D_MODEL,), 0.05)
    return {'x': x, 'norm1_w': norm1_w, 'w_in': w_in, 'ssd_conv_w': ssd_conv_w,
            'ssd_conv_b': ssd_conv_b, 'ssd_dt_bias': ssd_dt_bias, 'ssd_a_log': ssd_a_log,
            'ssd_d': ssd_d, 'ssd_norm_w': ssd_norm_w, 'gla_gate_w': gla_gate_w,
            'gla_gate_b': gla_gate_b, 'gla_norm_w': gla_norm_w, 'w_branch': w_branch,
            'w_out': w_out, 'norm2_w': norm2_w, 'ffn_up': ffn_up, 'ffn_conv_w': ffn_conv_w,
            'ffn_conv_b': ffn_conv_b, 'ffn_down': ffn_down, 'final_norm_w': final_norm_w}


def reference(x, norm1_w, w_in, ssd_conv_w, ssd_conv_b, ssd_dt_bias, ssd_a_log, ssd_d, ssd_norm_w,
              gla_gate_w, gla_gate_b, gla_norm_w, w_branch, w_out, norm2_w, ffn_up, ffn_conv_w,
              ffn_conv_b, ffn_down, final_norm_w):
    for l in range(DEPTH):
        x = x + hybrid_mixer(rmsnorm(x, norm1_w[l]), w_in[l], ssd_conv_w[l], ssd_conv_b[l],
                             ssd_dt_bias[l], ssd_a_log[l], ssd_d[l], ssd_norm_w[l],
                             gla_gate_w[l], gla_gate_b[l], gla_norm_w[l], w_branch[l], w_out[l])
        x = x + conv_ffn(rmsnorm(x, norm2_w[l]), ffn_up[l], ffn_conv_w[l], ffn_conv_b[l], ffn_down[l])
    return rmsnorm(x, final_norm_w)
```

```python
from contextlib import ExitStack
import numpy as np
import ml_dtypes
import concourse.bass as bass
import concourse.mybir as mybir
from concourse.bass_utils import run_bass_kernel_spmd

F32 = mybir.dt.float32
BF16 = mybir.dt.bfloat16
AF = mybir.ActivationFunctionType
ALU = mybir.AluOpType
AX = mybir.AxisListType

D = 1024
EPS = 1e-6
SAME_ENGINE_SYNC = True


class Res:
    __slots__ = ("name", "w", "rs", "dsem", "dcnt")

    def __init__(self, name):
        self.name = name
        self.w = {}
        self.rs = {}
        self.dsem = None
        self.dcnt = 0


class KB:
    def __init__(self, nc):
        self.nc = nc
        self.es = ExitStack()
        self.eng = {"pe": nc.tensor, "act": nc.scalar, "dve": nc.vector, "pool": nc.gpsimd,
                    "sp": nc.sync}
        self.sems = {}
        self.cnt = {}
        for e in ("pe", "act", "dve", "pool"):
            self.sems[e] = self.es.enter_context(nc.semaphore("s_" + e))
            self.cnt[e] = 0
        self.waited = {e: {} for e in self.eng}
        self.nsem = 0

    def sb(self, name, shape, dt):
        t = self.es.enter_context(self.nc.sbuf_tensor("sb_" + name, list(shape), dt))
        return t

    def ps(self, name, shape, dt=F32):
        return self.es.enter_context(self.nc.psum_tensor("ps_" + name, list(shape), dt))

    def res(self, name):
        return Res(name)

    def _dsem(self, r):
        if r.dsem is None:
            r.dsem = "d%d_%s" % (self.nsem, r.name)
            self.sems[r.dsem] = self.es.enter_context(self.nc.semaphore(r.dsem))
            self.nsem += 1
        return r.dsem

    def _deps(self, e, reads, writes):
        ev = {}
        for r in reads:
            for k, v in r.w.items():
                if ev.get(k, 0) < v:
                    ev[k] = v
        for w in writes:
            for k, v in w.w.items():
                if ev.get(k, 0) < v:
                    ev[k] = v
            for k, v in w.rs.items():
                if ev.get(k, 0) < v:
                    ev[k] = v
        wt = self.waited[e]
        for k, v in ev.items():
            if k == e and (e == "pe" or not SAME_ENGINE_SYNC):
                continue
            if wt.get(k, 0) >= v:
                continue
            self.eng[e].wait_ge(self.sems[k], v)
            wt[k] = v

    def op(self, e, reads, writes, fn):
        self._deps(e, reads, writes)
        ins = fn(self.eng[e])
        self.cnt[e] += 1
        ins.then_inc(self.sems[e], 1)
        evt = (e, self.cnt[e])
        for r in reads:
            if r.rs.get(e, 0) < evt[1]:
                r.rs[e] = evt[1]
        for w in writes:
            w.w = {evt[0]: evt[1]}
            w.rs = {}
        return ins

    def dma(self, q, out_ap, in_ap, reads, writes, owner, acc=False):
        self._deps(q, reads, writes)
        k = self._dsem(owner)
        ins = self.eng[q].dma_start(out=out_ap, in_=in_ap)
        owner.dcnt += 16
        ins.then_inc(self.sems[k], 16)
        evt = (k, owner.dcnt)
        for r in reads:
            if r.rs.get(k, 0) < evt[1]:
                r.rs[k] = evt[1]
        for w in writes:
            if acc:
                w.w[evt[0]] = evt[1]
            else:
                w.w = {evt[0]: evt[1]}
            w.rs = {}
        return ins

    def finish(self, q, res_list):
        self._deps(q, res_list, [])


class Rot:
    def __init__(self, items):
        self.items = items
        self.i = 0

    def next(self):
        it = self.items[self.i % len(self.items)]
        self.i += 1
        return it


FFN_H = 2816
NJ = FFN_H // 128


def build_T(NT, final):
    nc = bass.Bass("TRN2", target_bir_lowering=False)
    kb = KB(nc)
    TW = 512
    NTILE = NT // TW
    dr = lambda name, shape, dt, kind: nc.dram_tensor(name, list(shape), dt, kind=kind).ap()
    xT_d = dr("xT", [D, NT + 2], F32, "ExternalInput")
    yT_d = dr("yT", [2048, NT + 2], BF16, "ExternalInput")
    wg_d = dr("wg", [D, 3072], F32, "ExternalInput")
    wb_d = dr("wb", [2048, D], F32, "ExternalInput")
    wo_d = dr("wo", [D, D], F32, "ExternalInput")
    wup_d = dr("wup", [D, 2 * FFN_H], F32, "ExternalInput")
    wdn_d = dr("wdn", [FFN_H, D], F32, "ExternalInput")
    vec_d = dr("vecs", [128, 24 + 4 * NJ], F32, "ExternalInput")
    oT_d = dr("oT", [D, NT], F32, "ExternalOutput")
    wg_s = dr("wg_s", [24, 128, 8 * 128], BF16, "Internal")
    wb_s = dr("wb_s", [8, 128, 16 * 128], BF16, "Internal")
    wo_s = dr("wo_s", [8, 128, 8 * 128], BF16, "Internal")
    wup_s = dr("wup_s", [44, 128, 8 * 128], BF16, "Internal")
    wdn_s = dr("wdn_s", [8, 128, NJ * 128], BF16, "Internal")

    vecs = kb.sb("vecs", [128, 24 + 4 * NJ], F32); r_vecs = kb.res("vecs")
    ones = kb.sb("ones", [128, 128], BF16); r_ones = kb.res("ones")
    X = kb.sb("X", [128, 8, TW], F32); r_X = kb.res("X")
    Y = kb.sb("Y", [128, 16, TW], BF16); r_Y = kb.res("Y")
    SQ = kb.sb("SQ", [128, 8, TW], BF16); r_SQ = kb.res("SQ")
    RS = kb.sb("RS", [128, TW], F32); r_RS = kb.res("RS")
    H = kb.sb("H", [128, 8, TW], BF16); r_H = kb.res("H")
    G = kb.sb("G", [128, 24, TW], BF16); r_G = [kb.res("G%d" % i) for i in range(24)]
    MG = kb.sb("MG", [128, 8, TW], BF16); r_MG = [kb.res("MG%d" % i) for i in range(8)]
    TMP = [kb.sb("TMP%d" % i, [128, TW], F32) for i in range(2)]
    r_TMP = [kb.res("TMP%d" % i) for i in range(2)]
    GS = [kb.sb("GS%d" % i, [128, TW + 2], F32) for i in range(2)]
    r_GS = [kb.res("GS%d" % i) for i in range(2)]
    CA = [kb.sb("CA%d" % i, [128, TW], F32) for i in range(2)]
    r_CA = [kb.res("CA%d" % i) for i in range(2)]
    HALO = kb.sb("HALO", [128, NJ, 2], F32); r_HALO = [kb.res("HALO%d" % j) for j in range(NJ)]
    FH = kb.sb("FH", [128, NJ, TW], BF16); r_FH = [kb.res("FH%d" % j) for j in range(NJ)]
    NSL = 5
    SL = [kb.sb("SL%d" % i, [128, NJ * 128], BF16) for i in range(NSL)]
    r_SL = [kb.res("SL%d" % i) for i in range(NSL)]
    slabs = Rot(list(zip(SL, r_SL)))
    WST = [kb.sb("WST%d" % i, [128, 1024], F32) for i in range(2)]
    r_WST = [kb.res("WST%d" % i) for i in range(2)]
    WSB = [kb.sb("WSB%d" % i, [128, 1024], BF16) for i in range(2)]
    r_WSB = [kb.res("WSB%d" % i) for i in range(2)]
    NPS = 7
    PS = [kb.ps("PS%d" % i, [128, TW]) for i in range(NPS)]
    r_PS = [kb.res("PS%d" % i) for i in range(NPS)]
    psums = Rot(list(zip(PS, r_PS)))
    r_scr = {n: kb.res(n) for n in ("wg_s", "wb_s", "wo_s", "wup_s", "wdn_s")}
    r_out = kb.res("oT")

    n1 = lambda c: vecs[:, c:c + 1]
    n2 = lambda c: vecs[:, 8 + c:9 + c]
    nf = lambda c: vecs[:, 16 + c:17 + c]
    cw = lambda j, k: vecs[:, 24 + 3 * j + k:25 + 3 * j + k]
    cb = lambda j: vecs[:, 24 + 3 * NJ + j:25 + 3 * NJ + j]

    kb.dma("sp", vecs[:, :], vec_d[:, :], [], [r_vecs], r_vecs)
    kb.op("dve", [], [r_ones], lambda e: e.memset(ones[:, :], 1.0))
    kb.op("dve", [], r_HALO, lambda e: e.memset(HALO[:, :, :], 0.0))

    PW = 1024

    def convert(w_d, w_s, r_s, KC, N, idx=[0]):
        for kc in range(KC):
            for n0 in range(0, N, PW):
                n1_ = min(N, n0 + PW)
                nw = n1_ - n0
                i = idx[0] % 2
                idx[0] += 1
                kb.dma("sp", WST[i][:, 0:nw], w_d[kc * 128:(kc + 1) * 128, n0:n1_], [], [r_WST[i]], r_WST[i])
                eng = "pool" if (idx[0] % 2) else "dve"
                kb.op(eng, [r_WST[i]], [r_WSB[i]],
                      lambda e, i=i: e.tensor_copy(out=WSB[i][:, 0:nw], in_=WST[i][:, 0:nw]))
                kb.dma("pool", w_s[n0 // 128:n1_ // 128, :, kc * 128:(kc + 1) * 128].rearrange("c p n -> p c n"),
                       WSB[i][:, 0:nw].rearrange("p (c n) -> p c n", n=128),
                       [r_WSB[i]], [r_s], r_WSB[i], acc=True)

    convert(wg_d, wg_s, r_scr["wg_s"], 8, 3072)
    convert(wb_d, wb_s, r_scr["wb_s"], 16, D)
    convert(wo_d, wo_s, r_scr["wo_s"], 8, D)
    convert(wup_d, wup_s, r_scr["wup_s"], 8, 2 * FFN_H)
    convert(wdn_d, wdn_s, r_scr["wdn_s"], NJ, D)

    def load_slab(w_s, r_s, c, KC):
        sl, r_sl = slabs.next()
        kb.dma("sp", sl[:, 0:KC * 128], w_s[c, :, :], [r_s], [r_sl], r_sl)
        return sl, r_sl

    def norm(tw, nw, out_t, r_out_t, reads_x):
        kb.op("act", [r_X], [r_SQ], lambda e: e.activation(out=SQ[:, :, 0:tw], in_=X[:, :, 0:tw], func=AF.Square))
        ps, r_ps = psums.next()
        for c in range(8):
            kb.op("pe", [r_SQ, r_ones], [r_ps],
                  lambda e, c=c: e.matmul(ps[:, 0:tw], lhsT=ones[:, :], rhs=SQ[:, c, 0:tw], start=(c == 0), stop=(c == 7)))
        kb.op("act", [r_ps], [r_RS], lambda e: e.activation(out=RS[:, 0:tw], in_=ps[:, 0:tw], func=AF.Sqrt, bias=EPS, scale=1.0 / D))
        kb.op("dve", [r_RS], [r_RS], lambda e: e.reciprocal(out=RS[:, 0:tw], in_=RS[:, 0:tw]))
        for c in range(8):
            kb.op("dve", [r_X, r_RS, r_vecs], [r_out_t],
                  lambda e, c=c: e.scalar_tensor_tensor(out=out_t[:, c, 0:tw], in0=X[:, c, 0:tw], scalar=nw(c),
                                                        in1=RS[:, 0:tw], op0=ALU.mult, op1=ALU.mult))

    def tile(col0, tw, halo_only, out_col0):
        kb.dma("sp", X[:, :, 0:tw], xT_d[:, col0:col0 + tw].rearrange("(c p) t -> p c t", p=128), [], [r_X], r_X)
        kb.dma("sp", Y[:, :, 0:tw], yT_d[:, col0:col0 + tw].rearrange("(c p) t -> p c t", p=128), [], [r_Y], r_Y)
        norm(tw, n1, H, r_H, None)
        for gc in range(24):
            sl, r_sl = load_slab(wg_s, r_scr["wg_s"], gc, 8)
            ps, r_ps = psums.next()
            for kc in range(8):
                kb.op("pe", [r_sl, r_H], [r_ps],
                      lambda e, kc=kc: e.matmul(ps[:, 0:tw], lhsT=sl[:, kc * 128:(kc + 1) * 128], rhs=H[:, kc, 0:tw],
                                                start=(kc == 0), stop=(kc == 7)))
            kb.op("act", [r_ps], [r_G[gc]],
                  lambda e, gc=gc: e.activation(out=G[:, gc, 0:tw], in_=ps[:, 0:tw], func=AF.Sigmoid))
        for c in range(8):
            sl, r_sl = load_slab(wb_s, r_scr["wb_s"], c, 16)
            us = []
            for (k0, k1) in ((0, 8), (8, 12), (12, 16)):
                ps, r_ps = psums.next()
                for kc in range(k0, k1):
                    kb.op("pe", [r_sl, r_Y], [r_ps],
                          lambda e, kc=kc, k0=k0, k1=k1, ps=ps: e.matmul(
                              ps[:, 0:tw], lhsT=sl[:, kc * 128:(kc + 1) * 128], rhs=Y[:, kc, 0:tw],
                              start=(kc == k0), stop=(kc == k1 - 1)))
                us.append((ps, r_ps))
            t0, rt0 = TMP[0], r_TMP[0]
            t1, rt1 = TMP[1], r_TMP[1]
            kb.op("dve", [us[0][1], r_G[c]], [rt0],
                  lambda e: e.tensor_tensor(out=t0[:, 0:tw], in0=us[0][0][:, 0:tw], in1=G[:, c, 0:tw], op=ALU.mult))
            kb.op("dve", [us[1][1], r_G[8 + c]], [rt1],
                  lambda e: e.tensor_tensor(out=t1[:, 0:tw], in0=us[1][0][:, 0:tw], in1=G[:, 8 + c, 0:tw], op=ALU.mult))
            kb.op("pool", [rt0, rt1], [rt0],
                  lambda e: e.tensor_tensor(out=t0[:, 0:tw], in0=t0[:, 0:tw], in1=t1[:, 0:tw], op=ALU.add))
            kb.op("dve", [us[2][1], r_G[16 + c]], [rt1],
                  lambda e: e.tensor_tensor(out=t1[:, 0:tw], in0=us[2][0][:, 0:tw], in1=G[:, 16 + c, 0:tw], op=ALU.mult))
            kb.op("pool", [rt0, rt1], [r_MG[c]],
                  lambda e: e.tensor_tensor(out=MG[:, c, 0:tw], in0=t0[:, 0:tw], in1=t1[:, 0:tw], op=ALU.add))
        for c in range(8):
            sl, r_sl = load_slab(wo_s, r_scr["wo_s"], c, 8)
            ps, r_ps = psums.next()
            for kc in range(8):
                kb.op("pe", [r_sl, r_MG[kc]], [r_ps],
                      lambda e, kc=kc: e.matmul(ps[:, 0:tw], lhsT=sl[:, kc * 128:(kc + 1) * 128], rhs=MG[:, kc, 0:tw],
                                                start=(kc == 0), stop=(kc == 7)))
            kb.op("dve", [r_ps, r_X], [r_X],
                  lambda e: e.tensor_tensor(out=X[:, c, 0:tw], in0=ps[:, 0:tw], in1=X[:, c, 0:tw], op=ALU.add))
        norm(tw, n2, H, r_H, None)
        for j in range(NJ):
            slg, r_slg = load_slab(wup_s, r_scr["wup_s"], j, 8)
            psg, r_psg = psums.next()
            for kc in range(8):
                kb.op("pe", [r_slg, r_H], [r_psg],
                      lambda e, kc=kc: e.matmul(psg[:, 0:tw], lhsT=slg[:, kc * 128:(kc + 1) * 128], rhs=H[:, kc, 0:tw],
                                                start=(kc == 0), stop=(kc == 7)))
            gs, r_gs = GS[j % 2], r_GS[j % 2]
            kb.op("act", [r_HALO[j]], [r_gs], lambda e: e.copy(out=gs[:, 0:2], in_=HALO[:, j, :]))
            kb.op("act", [r_psg], [r_gs], lambda e: e.copy(out=gs[:, 2:2 + tw], in_=psg[:, 0:tw]))
            kb.op("act", [r_gs], [r_HALO[j]], lambda e: e.copy(out=HALO[:, j, :], in_=gs[:, tw:tw + 2]))
            if halo_only:
                continue
            slv, r_slv = load_slab(wup_s, r_scr["wup_s"], NJ + j, 8)
            psv, r_psv = psums.next()
            for kc in range(8):
                kb.op("pe", [r_slv, r_H], [r_psv],
                      lambda e, kc=kc: e.matmul(psv[:, 0:tw], lhsT=slv[:, kc * 128:(kc + 1) * 128], rhs=H[:, kc, 0:tw],
                                                start=(kc == 0), stop=(kc == 7)))
            ca, r_ca = CA[j % 2], r_CA[j % 2]
            kb.op("dve", [r_gs, r_vecs], [r_ca],
                  lambda e: e.tensor_scalar(out=ca[:, 0:tw], in0=gs[:, 0:tw], scalar1=cw(j, 0), scalar2=cb(j),
                                            op0=ALU.mult, op1=ALU.add))
            kb.op("dve", [r_gs, r_ca, r_vecs], [r_ca],
                  lambda e: e.scalar_tensor_tensor(out=ca[:, 0:tw], in0=gs[:, 1:1 + tw], scalar=cw(j, 1), in1=ca[:, 0:tw],
                                                   op0=ALU.mult, op1=ALU.add))
            kb.op("dve", [r_gs, r_ca, r_vecs], [r_ca],
                  lambda e: e.scalar_tensor_tensor(out=ca[:, 0:tw], in0=gs[:, 2:2 + tw], scalar=cw(j, 2), in1=ca[:, 0:tw],
                                                   op0=ALU.mult, op1=ALU.add))
            kb.op("act", [r_ca], [r_ca], lambda e: e.activation(out=ca[:, 0:tw], in_=ca[:, 0:tw], func=AF.Silu))
            kb.op("dve", [r_ca, r_psv], [r_FH[j]],
                  lambda e: e.tensor_tensor(out=FH[:, j, 0:tw], in0=psv[:, 0:tw], in1=ca[:, 0:tw], op=ALU.mult))
        if halo_only:
            return
        for c in range(8):
            sl, r_sl = load_slab(wdn_s, r_scr["wdn_s"], c, NJ)
            ps, r_ps = psums.next()
            for kc in range(NJ):
                kb.op("pe", [r_sl, r_FH[kc]], [r_ps],
                      lambda e, kc=kc: e.matmul(ps[:, 0:tw], lhsT=sl[:, kc * 128:(kc + 1) * 128], rhs=FH[:, kc, 0:tw],
                                                start=(kc == 0), stop=(kc == NJ - 1)))
            kb.op("dve", [r_ps, r_X], [r_X],
                  lambda e: e.tensor_tensor(out=X[:, c, 0:tw], in0=ps[:, 0:tw], in1=X[:, c, 0:tw], op=ALU.add))
        if final:
            norm(tw, nf, X, r_X, None)
        kb.dma("pool", oT_d[:, out_col0:out_col0 + tw].rearrange("(c p) t -> p c t", p=128), X[:, :, 0:tw],
               [r_X], [r_out], r_X, acc=True)

    tile(0, 2, True, 0)
    for i in range(NTILE):
        tile(2 + i * TW, TW, False, i * TW)
    kb.finish("pool", [r_out])
    kb.finish("sp", [r_out])
    return nc, kb


def pack_vecs_T(n1, n2, nf, cw, cb):
    v = np.zeros((128, 24 + 4 * NJ), np.float32)
    v[:, 0:8] = n1.reshape(8, 128).T
    v[:, 8:16] = n2.reshape(8, 128).T
    v[:, 16:24] = nf.reshape(8, 128).T
    v[:, 24:24 + 3 * NJ] = cw.reshape(3, NJ, 128).transpose(2, 1, 0).reshape(128, 3 * NJ)
    v[:, 24 + 3 * NJ:] = cb.reshape(NJ, 128).T
    return v


NWF = 2576
NWT = 580
NWALL = NWF + NWT
ATT_D = (1, 4, 16)
NEG = -30000.0
C_XS0, C_XS1, C_B, C_C, C_GQK, C_GLR, C_ATT = 0, 128, 256, 384, 512, 640, 656
V_N1, V_CW, V_CB, V_DTB, V_ALOG, V_DSK, V_SNW, V_GNW, NMV = 0, 8, 24, 28, 32, 36, 40, 296, 424
K_ID, K_U, K_LM, K_MA, K_MB, K_ONE, NCST = 0, 128, 256, 384, 640, 896, 1024


def build_M(S):
    nc = bass.Bass("TRN2", target_bir_lowering=False)
    kb = KB(nc)
    NTILE = S // 2048
    dr = lambda name, shape, dt, kind: nc.dram_tensor(name, list(shape), dt, kind=kind).ap()
    xT_d = dr("xT", [D, S], F32, "ExternalInput")
    w_d = dr("wall", [D, NWALL], F32, "ExternalInput")
    mv_d = dr("mv", [128, NMV], F32, "ExternalInput")
    gw_d = dr("gw", [17, 64], F32, "ExternalInput")
    cst_d = dr("cst", [128, NCST], F32, "ExternalInput")
    cos_d = dr("cosT", [128, S], F32, "ExternalInput")
    sin_d = dr("sinT", [128, S], F32, "ExternalInput")
    ysg_d = dr("ysg", [S, 384], BF16, "ExternalOutput")
    yat_d = dr("yat", [S, 128], BF16, "ExternalOutput")
    ao_d = dr("ao_s", [S, 3 * 130], F32, "Internal")
    r_ao = kb.res("ao_s")
    r_ysg = kb.res("ysg"); r_yat = kb.res("yat")

    def T(name, shape, dt):
        return kb.sb(name, shape, dt), kb.res(name)

    W, r_W = T("W", [128, 8, NWALL], BF16)
    MV, r_MV = T("MV", [128, NMV], F32)
    CST, r_CST = T("CST", [128, NCST], F32)
    IDB, r_IDB = T("IDB", [128, 128], BF16)
    ONB, r_ONB = T("ONB", [128, 128], BF16)
    GWF, r_GWF = T("GWF", [17, 64], F32)
    GWB, r_GWB = T("GWB", [17, 64], BF16)
    AN, r_AN = T("AN", [128, 4], F32)
    X, r_X = T("X", [128, 8, 256], F32)
    SQ, r_SQ = T("SQ", [128, 8, 256], BF16)
    RS, r_RS = T("RS", [128, 256], F32)
    HT, r_HT = T("HT", [128, 8, 512], BF16)
    PC = [T("PC%d" % i, [128, 515], F32) for i in range(4)]
    CAc = [T("CAc%d" % i, [128, 512], F32) for i in range(2)]
    XST = [T("XST%d" % i, [128, 512], BF16) for i in range(2)]
    BT, r_BT = T("BT", [128, 512], BF16)
    CT, r_CT = T("CT", [128, 512], BF16)
    GQ, r_GQ = T("GQ", [64, 512], F32)
    GK, r_GK = T("GK", [64, 512], F32)
    GLRT, r_GLRT = T("GLRT", [17, 512], BF16)
    COS, r_COS = T("COS", [128, 512], F32)
    SIN, r_SIN = T("SIN", [128, 512], F32)
    RT, r_RT = T("RT", [128, 512], F32)
    QT = [T("QT%d" % g, [128, 2048], BF16) for g in range(3)]
    KT = [T("KT%d" % g, [128, 2048 + 128 * ATT_D[g]], BF16) for g in range(3)]
    VT = [T("VT%d" % g, [128, 2048], BF16) for g in range(3)]
    VB = [T("VB%d" % g, [128, 16 + ATT_D[g], 128], BF16) for g in range(3)]
    YSG, r_YSG = T("YSG", [128, 4, 384], BF16)
    YA, r_YA = T("YA", [128, 16, 128], BF16)
    DTX, r_DTX = T("DTX", [128, 4], F32)
    DTA, r_DTA = T("DTA", [128, 4], F32)
    DT_, r_DT = T("DT", [128, 4], F32)
    AA, r_AA = T("AA", [128, 4], F32)
    ACS, r_ACS = T("ACS", [128, 4], F32)
    TE, r_TE = T("TE", [128, 4], F32)
    EA, r_EA = T("EA", [128, 4], F32)
    ETOT, r_ETOT = T("ETOT", [128, 4], F32)
    LH = [T("LH%d" % i, [128, 128], F32) for i in range(2)]
    ER, r_ER = T("ER", [128, 4, 128], F32)
    STM, r_STM = T("STM", [128, 128], F32)
    MT, r_MT = T("MT", [128, 4, 128], BF16)
    XTOK, r_XTOK = T("XTOK", [128, 4, 64], BF16)
    XDT, r_XDT = T("XDT", [128, 4, 64], BF16)
    XDE, r_XDE = T("XDE", [128, 4, 64], BF16)
    BTOK, r_BTOK = T("BTOK", [128, 128], BF16)
    YS, r_YS = T("YS", [128, 4, 64], F32)
    YD, r_YD = T("YD", [128, 4, 64], F32)
    SZ, r_SZ = T("SZ", [128, 256], F32)
    JK, r_JK = T("JK", [128, 256], F32)
    SSQ, r_SSQ = T("SSQ", [128, 1], F32)
    HS, r_HS = T("HS", [128, 4, 64], F32)
    HB, r_HB = T("HB", [128, 256], BF16)
    LT, r_LT = T("LT", [128, 64], F32)
    EQ, r_EQ = T("EQ", [64, 128], F32)
    EK, r_EK = T("EK", [64, 128], F32)
    EKT, r_EKT = T("EKT", [128, 64], F32)
    QTS, r_QTS = T("QTS", [64, 128], BF16)
    KTS, r_KTS = T("KTS", [64, 128], BF16)
    KTOK, r_KTOK = T("KTOK", [128, 64], BF16)
    VTOK, r_VTOK = T("VTOK", [128, 128], BF16)
    PTG, r_PTG = T("PTG", [128, 128], BF16)
    GS_, r_GS = T("GSst", [64, 128], F32)
    GSB, r_GSB = T("GSB", [64, 128], BF16)
    SR, r_SR = T("SR", [128, 128], F32)
    GO, r_GO = T("GO", [128, 128], F32)
    GSS, r_GSS = T("GSS", [128, 1], F32)
    SM, r_SM = T("SM", [128, 256], F32)
    MX, r_MX = T("MX", [128, 1], F32)
    NMX, r_NMX = T("NMX", [128, 1], F32)
    PB, r_PB = T("PB", [128, 256], BF16)
    PTB, r_PTB = T("PTB", [128, 256], BF16)
    AO = [T("AO%d" % i, [128, 130], F32) for i in range(2)]
    CM = [T("CM%d" % i, [128, 3, 130], F32) for i in range(2)]
    CE, r_CE = T("CE", [128, 8], F32)
    CACC, r_CACC = T("CACC", [128, 128], F32)

    B = [kb.ps("B%d" % i, [128, 512]) for i in range(7)]
    r_B = [kb.res("B%d" % i) for i in range(7)]
    B7 = kb.ps("B7", [128, 1024], BF16)
    r_B7 = kb.res("B7")
    P_ZD = (B[2][:, 0:260], r_B[2]); P_PRE = (B[2][:, 260:324], r_B[2]); P_CUM = (B[2][:, 324:388], r_B[2])
    P_SMALL = (B[2][:, 388:396], r_B[2])
    P_KVR = (B[3][:, 0:320], r_B[3]); P_SC = (B[3][:, 320:448], r_B[3])
    P_R = (B[4][:, :], r_B[4])
    P_ST = (B[5][:, 0:128], r_B[5]); P_Y = (B[5][:, 128:384], r_B[5]); P_CUMT = (B[5][0:64, 384:512], r_B[5])
    P_YO = (B[6][:, 0:256], r_B[6]); P_SN = (B[6][:, 256:512], r_B[6])
    P_GO = (B[1][:, 0:128], r_B[1]); P_GSN = (B[1][0:64, 128:256], r_B[1])
    P_AO = (B[1][:, 256:384], r_B[1])
    P_AS = (B[4][:, 0:256], r_B[4])
    P_TR = (B7[:, 0:384], r_B7); P_PT = (B7[:, 384:640], r_B7); P_VT = (B7[:, 640:768], r_B7)
    PA = (B[0][:, :], r_B[0])

    mvc = lambda a, b: MV[:, a:b]
    cst = lambda a, b: CST[:, a:b]

    kb.dma("sp", MV[:, :], mv_d[:, :], [], [r_MV], r_MV)
    kb.dma("sp", CST[:, :], cst_d[:, :], [], [r_CST], r_CST)
    kb.dma("sp", GWF[:, :], gw_d[:, :], [], [r_GWF], r_GWF)
    kb.op("dve", [r_CST], [r_IDB], lambda e: e.tensor_copy(out=IDB[:, :], in_=cst(K_ID, K_ID + 128)))
    kb.op("dve", [r_CST], [r_ONB], lambda e: e.tensor_copy(out=ONB[:, :], in_=cst(K_ONE, K_ONE + 128)))
    kb.op("dve", [r_GWF], [r_GWB], lambda e: e.tensor_copy(out=GWB[:, :], in_=GWF[:, :]))
    kb.op("act", [r_MV], [r_AN], lambda e: e.activation(out=AN[:, :], in_=mvc(V_ALOG, V_ALOG + 4), func=AF.Exp))
    kb.op("dve", [r_AN], [r_AN], lambda e: e.tensor_scalar(out=AN[:, :], in0=AN[:, :], scalar1=-1.0, scalar2=None, op0=ALU.mult))
    kb.op("dve", [], [r_GLRT], lambda e: e.memset(GLRT[:, :], 1.0))
    kb.op("dve", [], [r_HS], lambda e: e.memset(HS[:, :, :], 0.0))
    kb.op("dve", [], [r_HB], lambda e: e.memset(HB[:, :], 0.0))
    kb.op("dve", [], [r_GS], lambda e: e.memset(GS_[:, :], 0.0))
    kb.op("dve", [], [r_GSB], lambda e: e.memset(GSB[:, :], 0.0))
    for i in range(4):
        kb.op("pool", [], [PC[i][1]], lambda e: e.memset(PC[i][0][:, :], 0.0))
    for g in range(3):
        kb.op("pool", [], [KT[g][1]], lambda e: e.memset(KT[g][0][:, :], 0.0))
        kb.op("pool", [], [VB[g][1]], lambda e: e.memset(VB[g][0][:, :, :], 0.0))
    Xs = X[:, :, :].rearrange("p a b -> p (a b)")
    for kc in range(8):
        for n0 in range(0, NWALL, 2048):
            n1_ = min(NWALL, n0 + 2048)
            kb.dma("sp", Xs[:, 0:n1_ - n0], w_d[kc * 128:(kc + 1) * 128, n0:n1_], [], [r_X], r_X)
            kb.op("pool" if kc % 2 else "dve", [r_X], [r_W],
                  lambda e: e.tensor_copy(out=W[:, kc, n0:n1_], in_=Xs[:, 0:n1_ - n0]))

    def proj_fm(col, m, ncols, c0, dst):
        for kc in range(8):
            kb.op("pe", [r_W, r_HT], [dst[1]],
                  lambda e: e.matmul(dst[0], lhsT=W[:, kc, col:col + m], rhs=HT[:, kc, c0:c0 + ncols],
                                     start=(kc == 0), stop=(kc == 7)))

    def ssd_chunk(co, ych):
        cs = slice(co, co + 128)
        for kc in range(8):
            kb.op("pe", [r_W, r_HT], [P_ZD[1]],
                  lambda e: e.matmul(P_ZD[0], lhsT=HT[:, kc, cs], rhs=W[:, kc, NWF:NWF + 260], start=(kc == 0), stop=(kc == 7)))
        zd = B[2]
        kb.op("dve", [P_ZD[1], r_MV], [r_DTX], lambda e: e.tensor_tensor(out=DTX[:, :], in0=zd[:, 256:260], in1=mvc(V_DTB, V_DTB + 4), op=ALU.add))
        kb.op("act", [r_DTX], [r_DTA], lambda e: e.activation(out=DTA[:, :], in_=DTX[:, :], func=AF.Abs))
        kb.op("act", [r_DTA], [r_DTA], lambda e: e.activation(out=DTA[:, :], in_=DTA[:, :], func=AF.Exp, scale=-1.0))
        kb.op("act", [r_DTA], [r_DTA], lambda e: e.activation(out=DTA[:, :], in_=DTA[:, :], func=AF.Ln, bias=1.0))
        kb.op("dve", [r_DTX, r_DTA], [r_DT], lambda e: e.scalar_tensor_tensor(out=DT_[:, :], in0=DTX[:, :], scalar=0.0, in1=DTA[:, :], op0=ALU.max, op1=ALU.add))
        kb.op("dve", [r_DT, r_AN], [r_AA], lambda e: e.tensor_tensor(out=AA[:, :], in0=DT_[:, :], in1=AN[:, :], op=ALU.mult))
        sm = B[2]
        kb.op("pe", [r_CST, r_AA], [P_SMALL[1]], lambda e: e.matmul(sm[:, 388:392], lhsT=cst(K_U, K_U + 128), rhs=AA[:, :], start=True, stop=True))
        kb.op("pe", [r_CST, r_AA], [P_SMALL[1]], lambda e: e.matmul(sm[:, 392:396], lhsT=cst(K_ONE, K_ONE + 128), rhs=AA[:, :], start=True, stop=True))
        kb.op("act", [P_SMALL[1]], [r_ACS], lambda e: e.copy(out=ACS[:, :], in_=sm[:, 388:392]))
        kb.op("act", [r_ACS], [r_EA], lambda e: e.activation(out=EA[:, :], in_=ACS[:, :], func=AF.Exp))
        kb.op("act", [P_SMALL[1]], [r_ETOT], lambda e: e.activation(out=ETOT[:, :], in_=sm[:, 392:396], func=AF.Exp))
        kb.op("dve", [P_SMALL[1], r_ACS], [r_TE], lambda e: e.tensor_tensor(out=TE[:, :], in0=sm[:, 392:396], in1=ACS[:, :], op=ALU.subtract))
        kb.op("act", [r_TE], [r_TE], lambda e: e.activation(out=TE[:, :], in_=TE[:, :], func=AF.Exp))
        for h in range(4):
            lh, r_lh = LH[h % 2]
            kb.op("dve", [r_CST, r_AA], [r_lh], lambda e: e.tensor_scalar(out=lh[:, :], in0=cst(K_LM, K_LM + 128), scalar1=AA[:, h:h + 1], scalar2=None, op0=ALU.mult))
            kb.op("pe", [r_lh, r_CST], [P_R[1]], lambda e: e.matmul(B[4][:, h * 128:(h + 1) * 128], lhsT=lh[:, :], rhs=cst(K_U, K_U + 128), start=True, stop=True))
        kb.op("act", [P_R[1]], [r_ER], lambda e: e.activation(out=ER[:, :, :].rearrange("p a b -> p (a b)"), in_=B[4][:, :], func=AF.Exp))
        kb.op("pe", [r_BT, r_CT], [P_ST[1]], lambda e: e.matmul(P_ST[0], lhsT=BT[:, cs], rhs=CT[:, cs], start=True, stop=True))
        kb.op("dve", [P_ST[1], r_CST], [r_STM], lambda e: e.tensor_tensor(out=STM[:, :], in0=P_ST[0], in1=cst(K_U, K_U + 128), op=ALU.mult))
        kb.op("dve", [r_STM, r_ER], [r_MT], lambda e: e.tensor_tensor(out=MT[:, :, :], in0=ER[:, :, :], in1=STM[:, :].unsqueeze(1).to_broadcast([128, 4, 128]), op=ALU.mult))
        kb.op("pe", [XST[0][1], r_IDB], [P_TR[1]], lambda e: e.transpose(B7[:, 0:128], XST[0][0][:, cs], IDB[:, :]))
        kb.op("pe", [XST[1][1], r_IDB], [P_TR[1]], lambda e: e.transpose(B7[:, 128:256], XST[1][0][:, cs], IDB[:, :]))
        kb.op("pe", [r_BT, r_IDB], [P_TR[1]], lambda e: e.transpose(B7[:, 256:384], BT[:, cs], IDB[:, :]))
        xtp = B7[:, 0:256].rearrange("p (h d) -> p h d", h=4)
        kb.op("act", [P_TR[1]], [r_XTOK], lambda e: e.copy(out=XTOK[:, :, :], in_=xtp))
        kb.op("act", [P_TR[1]], [r_BTOK], lambda e: e.copy(out=BTOK[:, :], in_=B7[:, 256:384]))
        kb.op("dve", [r_XTOK, r_DT], [r_XDT], lambda e: e.tensor_tensor(out=XDT[:, :, :], in0=XTOK[:, :, :], in1=DT_[:, :].unsqueeze(2).to_broadcast([128, 4, 64]), op=ALU.mult))
        kb.op("dve", [r_XDT, r_TE], [r_XDE], lambda e: e.tensor_tensor(out=XDE[:, :, :], in0=XDT[:, :, :], in1=TE[:, :].unsqueeze(2).to_broadcast([128, 4, 64]), op=ALU.mult))
        for h in range(4):
            kb.op("pe", [r_MT, r_XDT], [P_Y[1]], lambda e: e.matmul(B[5][:, 128 + h * 64:128 + (h + 1) * 64], lhsT=MT[:, h, :], rhs=XDT[:, h, :], start=True, stop=True))
        kb.op("pe", [r_CT, r_HB], [P_YO[1]], lambda e: e.matmul(P_YO[0], lhsT=CT[:, cs], rhs=HB[:, :], start=True, stop=True))
        kb.op("dve", [P_YO[1], r_EA], [r_YS], lambda e: e.tensor_tensor(out=YS[:, :, :], in0=P_YO[0].rearrange("p (h d) -> p h d", h=4), in1=EA[:, :].unsqueeze(2).to_broadcast([128, 4, 64]), op=ALU.mult))
        kb.op("dve", [P_Y[1], r_YS], [r_YS], lambda e: e.tensor_tensor(out=YS[:, :, :], in0=P_Y[0].rearrange("p (h d) -> p h d", h=4), in1=YS[:, :, :], op=ALU.add))
        kb.op("pool", [r_XTOK, r_MV], [r_YD], lambda e: e.tensor_tensor(out=YD[:, :, :], in0=XTOK[:, :, :], in1=mvc(V_DSK, V_DSK + 4).unsqueeze(2).to_broadcast([128, 4, 64]), op=ALU.mult))
        kb.op("dve", [r_YD, r_YS], [r_YS], lambda e: e.tensor_tensor(out=YS[:, :, :], in0=YS[:, :, :], in1=YD[:, :, :], op=ALU.add))
        kb.op("pe", [r_BTOK, r_XDE], [P_SN[1]], lambda e: e.matmul(P_SN[0], lhsT=BTOK[:, :], rhs=XDE[:, :, :].rearrange("p h d -> p (h d)"), start=True, stop=True))
        kb.op("dve", [r_HS, r_ETOT], [r_HS], lambda e: e.tensor_tensor(out=HS[:, :, :], in0=HS[:, :, :], in1=ETOT[:, :].unsqueeze(2).to_broadcast([128, 4, 64]), op=ALU.mult))
        kb.op("dve", [r_HS, P_SN[1]], [r_HS], lambda e: e.tensor_tensor(out=HS[:, :, :], in0=P_SN[0].rearrange("p (h d) -> p h d", h=4), in1=HS[:, :, :], op=ALU.add))
        kb.op("act", [r_HS], [r_HB], lambda e: e.copy(out=HB[:, :], in_=HS[:, :, :].rearrange("p h d -> p (h d)")))
        kb.op("act", [P_ZD[1]], [r_SZ], lambda e: e.activation(out=SZ[:, :], in_=zd[:, 0:256], func=AF.Silu))
        ysf = YS[:, :, :].rearrange("p h d -> p (h d)")
        kb.op("dve", [r_YS, r_SZ], [r_SZ], lambda e: e.tensor_tensor(out=SZ[:, :], in0=ysf, in1=SZ[:, :], op=ALU.mult))
        kb.op("act", [r_SZ], [r_JK, r_SSQ], lambda e: e.activation(out=JK[:, :], in_=SZ[:, :], func=AF.Square, accum_out=SSQ[:, :]))
        kb.op("act", [r_SSQ], [r_SSQ], lambda e: e.activation(out=SSQ[:, :], in_=SSQ[:, :], func=AF.Sqrt, bias=EPS, scale=1.0 / 256))
        kb.op("dve", [r_SSQ], [r_SSQ], lambda e: e.reciprocal(out=SSQ[:, :], in_=SSQ[:, :]))
        kb.op("dve", [r_SZ, r_SSQ, r_MV], [r_YSG], lambda e: e.scalar_tensor_tensor(out=YSG[:, ych, 0:256], in0=SZ[:, :], scalar=SSQ[:, 0:1], in1=mvc(V_SNW, V_SNW + 256), op0=ALU.mult, op1=ALU.mult))

    def gla_chunk(co, ych):
        cs = slice(co, co + 128)
        for kc in range(8):
            kb.op("pe", [r_W, r_HT], [P_KVR[1]],
                  lambda e: e.matmul(P_KVR[0], lhsT=HT[:, kc, cs], rhs=W[:, kc, NWF + 260:NWF + 580], start=(kc == 0), stop=(kc == 7)))
        kvr = B[3]
        kb.op("pe", [r_GLRT, r_GWB], [P_PRE[1]], lambda e: e.matmul(P_PRE[0], lhsT=GLRT[:, cs], rhs=GWB[:, :], start=True, stop=True))
        kb.op("act", [P_PRE[1]], [r_LT], lambda e: e.activation(out=LT[:, :], in_=P_PRE[0], func=AF.Exp, scale=-1.0))
        kb.op("act", [r_LT], [r_LT], lambda e: e.activation(out=LT[:, :], in_=LT[:, :], func=AF.Ln, bias=1.0))
        kb.op("pe", [r_LT, r_CST], [P_CUMT[1]], lambda e: e.matmul(P_CUMT[0], lhsT=LT[:, :], rhs=cst(K_U, K_U + 128), start=True, stop=True))
        kb.op("pe", [r_LT, r_CST], [P_CUM[1]], lambda e: e.matmul(P_CUM[0], lhsT=cst(K_U, K_U + 128), rhs=LT[:, :], start=True, stop=True))
        kb.op("act", [P_CUMT[1]], [r_EQ], lambda e: e.activation(out=EQ[:, :], in_=P_CUMT[0], func=AF.Exp, scale=-1.0 / 16))
        kb.op("act", [P_CUMT[1]], [r_EK], lambda e: e.activation(out=EK[:, :], in_=P_CUMT[0], func=AF.Exp, scale=1.0 / 16))
        kb.op("act", [P_CUM[1]], [r_EKT], lambda e: e.activation(out=EKT[:, :], in_=P_CUM[0], func=AF.Exp, scale=1.0 / 16))
        kb.op("dve", [r_GQ, r_EQ], [r_QTS], lambda e: e.scalar_tensor_tensor(out=QTS[:, :], in0=GQ[:, cs], scalar=0.125, in1=EQ[:, :], op0=ALU.mult, op1=ALU.mult))
        kb.op("dve", [r_GK, r_EK], [r_KTS], lambda e: e.tensor_tensor(out=KTS[:, :], in0=GK[:, cs], in1=EK[:, :], op=ALU.mult))
        kb.op("dve", [P_KVR[1], r_EKT], [r_KTOK], lambda e: e.tensor_tensor(out=KTOK[:, :], in0=kvr[:, 0:64], in1=EKT[:, :], op=ALU.mult))
        kb.op("act", [P_KVR[1]], [r_VTOK], lambda e: e.copy(out=VTOK[:, :], in_=kvr[:, 64:192]))
        kb.op("pe", [r_KTS, r_QTS], [P_SC[1]], lambda e: e.matmul(P_SC[0], lhsT=KTS[:, :], rhs=QTS[:, :], start=True, stop=True))
        kb.op("dve", [P_SC[1], r_CST], [r_PTG], lambda e: e.tensor_tensor(out=PTG[:, :], in0=P_SC[0], in1=cst(K_U, K_U + 128), op=ALU.mult))
        kb.op("pe", [r_PTG, r_VTOK], [P_GO[1]], lambda e: e.matmul(P_GO[0], lhsT=PTG[:, :], rhs=VTOK[:, :], start=True, stop=False))
        kb.op("pe", [r_QTS, r_GSB], [P_GO[1]], lambda e: e.matmul(P_GO[0], lhsT=QTS[:, :], rhs=GSB[:, :], start=False, stop=True))
        kb.op("pe", [r_KTOK, r_VTOK], [P_GSN[1]], lambda e: e.matmul(P_GSN[0], lhsT=KTOK[:, :], rhs=VTOK[:, :], start=True, stop=True))
        kb.op("dve", [P_GSN[1], r_GS], [r_GS], lambda e: e.tensor_tensor(out=GS_[:, :], in0=P_GSN[0], in1=GS_[:, :], op=ALU.add))
        kb.op("dve", [r_GS, r_EQ], [r_GS], lambda e: e.tensor_scalar(out=GS_[:, :], in0=GS_[:, :], scalar1=EQ[:, 127:128], scalar2=None, op0=ALU.mult))
        kb.op("act", [r_GS], [r_GSB], lambda e: e.copy(out=GSB[:, :], in_=GS_[:, :]))
        kb.op("act", [P_GO[1]], [r_GO, r_GSS], lambda e: e.activation(out=GO[:, :], in_=P_GO[0], func=AF.Square, accum_out=GSS[:, :]))
        kb.op("act", [r_GSS], [r_GSS], lambda e: e.activation(out=GSS[:, :], in_=GSS[:, :], func=AF.Sqrt, bias=EPS, scale=1.0 / 128))
        kb.op("dve", [r_GSS], [r_GSS], lambda e: e.reciprocal(out=GSS[:, :], in_=GSS[:, :]))
        kb.op("act", [P_KVR[1]], [r_SR], lambda e: e.activation(out=SR[:, :], in_=kvr[:, 192:320], func=AF.Silu))
        kb.op("dve", [P_GO[1], r_GSS, r_MV], [r_GO], lambda e: e.scalar_tensor_tensor(out=GO[:, :], in0=P_GO[0], scalar=GSS[:, 0:1], in1=mvc(V_GNW, V_GNW + 128), op0=ALU.mult, op1=ALU.mult))
        kb.op("dve", [r_GO, r_SR], [r_YSG], lambda e: e.tensor_tensor(out=YSG[:, ych, 256:384], in0=GO[:, :], in1=SR[:, :], op=ALU.mult))

    def att_block(g, blk, first_tile):
        d = ATT_D[g]
        s, r = blk // d, blk % d
        base = s * 128 * d + r
        qsl = slice(base, base + 127 * d + 1, d)
        kcur = slice(128 * d + base, 128 * d + base + 127 * d + 1, d)
        kprev = slice(base, base + 127 * d + 1, d)
        qt, r_qt = QT[g]; kt, r_kt = KT[g]; vb, r_vb = VB[g]
        nopast = first_tile and s == 0
        kb.op("pe", [r_qt, r_kt], [P_AS[1]], lambda e: e.matmul(B[4][:, 0:128], lhsT=qt[:, qsl], rhs=kt[:, kprev], start=True, stop=True))
        kb.op("pe", [r_qt, r_kt], [P_AS[1]], lambda e: e.matmul(B[4][:, 128:256], lhsT=qt[:, qsl], rhs=kt[:, kcur], start=True, stop=True))
        mk = K_MB if nopast else K_MA
        kb.op("dve", [P_AS[1], r_CST], [r_SM], lambda e: e.tensor_tensor(out=SM[:, :], in0=P_AS[0], in1=cst(mk, mk + 256), op=ALU.add))
        kb.op("dve", [r_SM], [r_MX], lambda e: e.reduce_max(out=MX[:, :], in_=SM[:, :], axis=AX.X))
        sc = 128.0 ** -0.5
        kb.op("dve", [r_MX], [r_NMX], lambda e: e.tensor_scalar(out=NMX[:, :], in0=MX[:, :], scalar1=-sc, scalar2=None, op0=ALU.mult))
        ao, r_ao_sb = AO[blk % 2]
        kb.op("act", [r_SM, r_NMX], [r_PB, r_ao_sb], lambda e: e.activation(out=PB[:, :], in_=SM[:, :], func=AF.Exp, bias=NMX[:, 0:1], scale=sc, accum_out=ao[:, 129:130]))
        kb.op("pe", [r_PB, r_IDB], [P_PT[1]], lambda e: e.transpose(B7[:, 384:512], PB[:, 0:128], IDB[:, :]))
        kb.op("pe", [r_PB, r_IDB], [P_PT[1]], lambda e: e.transpose(B7[:, 512:640], PB[:, 128:256], IDB[:, :]))
        kb.op("act", [P_PT[1]], [r_PTB], lambda e: e.copy(out=PTB[:, :], in_=B7[:, 384:640]))
        kb.op("pe", [r_PTB, r_vb], [P_AO[1]], lambda e: e.matmul(P_AO[0], lhsT=PTB[:, 0:128], rhs=vb[:, blk, :], start=True, stop=False))
        kb.op("pe", [r_PTB, r_vb], [P_AO[1]], lambda e: e.matmul(P_AO[0], lhsT=PTB[:, 128:256], rhs=vb[:, blk + d, :], start=False, stop=True))
        kb.op("act", [P_AO[1]], [r_ao_sb], lambda e: e.copy(out=ao[:, 0:128], in_=P_AO[0]))
        kb.op("dve", [r_MX], [r_ao_sb], lambda e: e.tensor_scalar(out=ao[:, 128:129], in0=MX[:, :], scalar1=sc, scalar2=None, op0=ALU.mult))
        return ao, r_ao_sb, base, d

    RT2, r_RT2 = T("RT2", [128, 512], F32)

    def rope_proj(g, c0):
        qt, r_qt = QT[g]; kt, r_kt = KT[g]; vt, r_vt = VT[g]
        d = ATT_D[g]
        cb = C_ATT + g * 640
        for which, (dst, r_dst, off) in enumerate(((qt, r_qt, c0), (kt, r_kt, 128 * d + c0))):
            col = cb + which * 256
            proj_fm(col, 128, 512, 0, PA)
            kb.op("dve", [PA[1], r_COS], [r_RT], lambda e: e.tensor_tensor(out=RT[:, :], in0=PA[0], in1=COS[:, :], op=ALU.mult))
            proj_fm(col + 128, 128, 512, 0, PA)
            kb.op("dve", [PA[1], r_SIN], [r_RT2], lambda e: e.tensor_tensor(out=RT2[:, :], in0=PA[0], in1=SIN[:, :], op=ALU.mult))
            kb.op("pool", [r_RT, r_RT2], [r_dst], lambda e: e.tensor_tensor(out=dst[:, off:off + 512], in0=RT[:, :], in1=RT2[:, :], op=ALU.add))
        proj_fm(cb + 512, 128, 512, 0, PA)
        kb.op("act", [PA[1]], [r_vt], lambda e: e.copy(out=vt[:, c0:c0 + 512], in_=PA[0]))

    for ti in range(NTILE):
        t0 = ti * 2048
        for u in range(4):
            c0 = t0 + u * 512
            for hf in range(2):
                cc = c0 + hf * 256
                kb.dma("sp", X[:, :, :], xT_d[:, cc:cc + 256].rearrange("(c p) t -> p c t", p=128), [], [r_X], r_X)
                kb.op("act", [r_X], [r_SQ], lambda e: e.activation(out=SQ[:, :, :], in_=X[:, :, :], func=AF.Square))
                for kc in range(8):
                    kb.op("pe", [r_SQ, r_ONB], [PA[1]], lambda e: e.matmul(B[0][:, 0:256], lhsT=ONB[:, :], rhs=SQ[:, kc, :], start=(kc == 0), stop=(kc == 7)))
                kb.op("act", [PA[1]], [r_RS], lambda e: e.activation(out=RS[:, :], in_=B[0][:, 0:256], func=AF.Sqrt, bias=EPS, scale=1.0 / D))
                kb.op("dve", [r_RS], [r_RS], lambda e: e.reciprocal(out=RS[:, :], in_=RS[:, :]))
                for kc in range(8):
                    kb.op("dve", [r_X, r_RS, r_MV], [r_HT],
                          lambda e: e.scalar_tensor_tensor(out=HT[:, kc, hf * 256:(hf + 1) * 256], in0=X[:, kc, :], scalar=MV[:, V_N1 + kc:V_N1 + kc + 1],
                                                           in1=RS[:, :], op0=ALU.mult, op1=ALU.mult))
            kb.dma("sp", COS[:, :], cos_d[:, c0:c0 + 512], [], [r_COS], r_COS)
            kb.dma("sp", SIN[:, :], sin_d[:, c0:c0 + 512], [], [r_SIN], r_SIN)
            for i, col in enumerate((C_XS0, C_XS1, C_B, C_C)):
                pc, r_pc = PC[i]
                proj_fm(col, 128, 512, 0, PA)
                kb.op("act", [r_pc], [r_pc], lambda e: e.copy(out=pc[:, 0:3], in_=pc[:, 512:515]))
                kb.op("act", [PA[1]], [r_pc], lambda e: e.copy(out=pc[:, 3:515], in_=PA[0]))
                ca, r_ca = CAc[i % 2]
                cwv = lambda k: MV[:, V_CW + 4 * i + k:V_CW + 4 * i + k + 1]
                kb.op("dve", [r_pc, r_MV], [r_ca], lambda e: e.tensor_scalar(out=ca[:, :], in0=pc[:, 0:512], scalar1=cwv(0), scalar2=MV[:, V_CB + i:V_CB + i + 1], op0=ALU.mult, op1=ALU.add))
                for k in (1, 2, 3):
                    kb.op("dve", [r_pc, r_ca, r_MV], [r_ca], lambda e: e.scalar_tensor_tensor(out=ca[:, :], in0=pc[:, k:k + 512], scalar=cwv(k), in1=ca[:, :], op0=ALU.mult, op1=ALU.add))
                dst, r_dst = (XST[0], XST[1], (BT, r_BT), (CT, r_CT))[i]
                kb.op("act", [r_ca], [r_dst], lambda e: e.activation(out=dst[:, :], in_=ca[:, :], func=AF.Silu))
            proj_fm(C_GQK, 64, 512, 0, (B[0][0:64, :], PA[1]))
            kb.op("act", [PA[1]], [r_GQ], lambda e: e.copy(out=GQ[:, :], in_=B[0][0:64, :]))
            proj_fm(C_GQK + 64, 64, 512, 0, (B[0][0:64, :], PA[1]))
            kb.op("act", [PA[1]], [r_GK], lambda e: e.copy(out=GK[:, :], in_=B[0][0:64, :]))
            proj_fm(C_GLR, 16, 512, 0, (B[0][0:16, :], PA[1]))
            kb.op("act", [PA[1]], [r_GLRT], lambda e: e.copy(out=GLRT[0:16, :], in_=B[0][0:16, :]))
            for ci in range(4):
                ssd_chunk(ci * 128, ci)
                gla_chunk(ci * 128, ci)
            kb.dma("pool", ysg_d[c0:c0 + 512, :].rearrange("(c p) n -> p c n", p=128), YSG[:, :, :], [r_YSG], [r_ysg], r_YSG, acc=True)
            for g in range(3):
                rope_proj(g, u * 512)
        for g in range(3):
            d = ATT_D[g]
            vt, r_vt = VT[g]; vb, r_vb = VB[g]; kt, r_kt = KT[g]
            for blk in range(16):
                s, r = blk // d, blk % d
                base = s * 128 * d + r
                vsl = slice(base, base + 127 * d + 1, d)
                kb.op("pe", [r_vt, r_IDB], [P_VT[1]], lambda e: e.transpose(P_VT[0], vt[:, vsl], IDB[:, :]))
                kb.op("act", [P_VT[1]], [r_vb], lambda e: e.copy(out=vb[:, blk + d, :], in_=P_VT[0]))
            for blk in range(16):
                ao, r_ao_sb, base, d = att_block(g, blk, ti == 0)
                tok0 = t0 + base
                rows = ao_d[tok0:tok0 + 127 * d + 1:d, g * 130:(g + 1) * 130]
                kb.dma("sp", rows, ao[:, :], [r_ao_sb], [r_ao], r_ao_sb, acc=True)
            kb.op("pool", [r_kt], [r_kt], lambda e: e.tensor_copy(out=kt[:, 0:128 * d], in_=kt[:, 2048:2048 + 128 * d]))
            kb.op("pool", [r_vb], [r_vb], lambda e: e.tensor_copy(out=vb[:, 0:d, :], in_=vb[:, 16:16 + d, :]))
        for ch in range(16):
            cm, r_cm = CM[ch % 2]
            tok0 = t0 + ch * 128
            kb.dma("sp", cm[:, :, :].rearrange("p g n -> p (g n)"), ao_d[tok0:tok0 + 128, :], [r_ao], [r_cm], r_cm)
            mcol = cm[:, :, 128]
            lcol = cm[:, :, 129]
            kb.op("dve", [r_cm], [r_CE], lambda e: e.tensor_tensor(out=CE[:, 0:1], in0=cm[:, 0, 128:129], in1=cm[:, 1, 128:129], op=ALU.max))
            kb.op("dve", [r_cm, r_CE], [r_CE], lambda e: e.tensor_tensor(out=CE[:, 0:1], in0=CE[:, 0:1], in1=cm[:, 2, 128:129], op=ALU.max))
            kb.op("dve", [r_cm, r_CE], [r_CE], lambda e: e.tensor_scalar(out=CE[:, 1:4], in0=mcol, scalar1=CE[:, 0:1], scalar2=None, op0=ALU.subtract))
            kb.op("act", [r_CE], [r_CE], lambda e: e.activation(out=CE[:, 1:4], in_=CE[:, 1:4], func=AF.Exp))
            kb.op("dve", [r_cm, r_CE], [r_CE], lambda e: e.tensor_tensor(out=CE[:, 4:7], in0=CE[:, 1:4], in1=lcol, op=ALU.mult))
            kb.op("dve", [r_CE], [r_CE], lambda e: e.reduce_sum(out=CE[:, 7:8], in_=CE[:, 4:7], axis=AX.X))
            kb.op("dve", [r_CE], [r_CE], lambda e: e.reciprocal(out=CE[:, 7:8], in_=CE[:, 7:8]))
            kb.op("dve", [r_CE], [r_CE], lambda e: e.tensor_scalar(out=CE[:, 1:4], in0=CE[:, 1:4], scalar1=CE[:, 7:8], scalar2=None, op0=ALU.mult))
            kb.op("dve", [r_cm, r_CE], [r_CACC], lambda e: e.tensor_scalar(out=CACC[:, :], in0=cm[:, 0, 0:128], scalar1=CE[:, 1:2], scalar2=None, op0=ALU.mult))
            kb.op("dve", [r_cm, r_CE, r_CACC], [r_CACC], lambda e: e.scalar_tensor_tensor(out=CACC[:, :], in0=cm[:, 1, 0:128], scalar=CE[:, 2:3], in1=CACC[:, :], op0=ALU.mult, op1=ALU.add))
            kb.op("dve", [r_cm, r_CE, r_CACC], [r_YA], lambda e: e.scalar_tensor_tensor(out=YA[:, ch, :], in0=cm[:, 2, 0:128], scalar=CE[:, 3:4], in1=CACC[:, :], op0=ALU.mult, op1=ALU.add))
        kb.dma("pool", yat_d[t0:t0 + 2048, :].rearrange("(c p) n -> p c n", p=128), YA[:, :, :], [r_YA], [r_yat], r_YA, acc=True)
    kb.finish("pool", [r_ysg, r_yat])
    kb.finish("sp", [r_ysg, r_yat])
    return nc, kb


def m_wall(w_in, j):
    w = np.zeros((D, NWALL), np.float32)
    xs0 = 1024 + j * 256
    w[:, 0:256] = w_in[:, xs0:xs0 + 256]
    w[:, 256:384] = w_in[:, 2048 + j * 128:2048 + (j + 1) * 128]
    w[:, 384:512] = w_in[:, 2560 + j * 128:2560 + (j + 1) * 128]
    w[:, 512:576] = w_in[:, 3088 + j * 64:3088 + (j + 1) * 64]
    w[:, 576:640] = w_in[:, 3344 + j * 64:3344 + (j + 1) * 64]
    w[:, 640:656] = w_in[:, 4112:4128]
    for g in range(3):
        cb = C_ATT + g * 640
        hd = (g * 4 + j) * 128
        for which in range(2):
            c = 4640 + which * 1536 + hd
            w[:, cb + which * 256:cb + which * 256 + 128] = w_in[:, c:c + 128]
            w[:, cb + which * 256 + 128:cb + which * 256 + 192] = w_in[:, c + 64:c + 128]
            w[:, cb + which * 256 + 192:cb + which * 256 + 256] = w_in[:, c:c + 64]
        c = 4640 + 3072 + hd
        w[:, cb + 512:cb + 640] = w_in[:, c:c + 128]
    o = NWF
    w[:, o:o + 256] = w_in[:, j * 256:(j + 1) * 256]
    w[:, o + 256:o + 260] = w_in[:, 3072 + j * 4:3072 + (j + 1) * 4]
    w[:, o + 260:o + 324] = w_in[:, 3344 + j * 64:3344 + (j + 1) * 64]
    w[:, o + 324:o + 452] = w_in[:, 3600 + j * 128:3600 + (j + 1) * 128]
    w[:, o + 452:o + 580] = w_in[:, 4128 + j * 128:4128 + (j + 1) * 128]
    return w


def m_mv(n1, conv_w, conv_b, dt_bias, a_log, dsk, ssd_nw, gla_nw, j):
    v = np.zeros((128, NMV), np.float32)
    v[:, V_N1:V_N1 + 8] = n1.reshape(8, 128).T
    chs = [j * 256, j * 256 + 128, 1024 + j * 128, 1536 + j * 128]
    for i, c in enumerate(chs):
        v[:, V_CW + 4 * i:V_CW + 4 * i + 4] = conv_w[:, c:c + 128].T
        v[:, V_CB + i] = conv_b[c:c + 128]
    v[:, V_DTB:V_DTB + 4] = dt_bias[4 * j:4 * j + 4][None, :]
    v[:, V_ALOG:V_ALOG + 4] = a_log[4 * j:4 * j + 4][None, :]
    v[:, V_DSK:V_DSK + 4] = dsk[4 * j:4 * j + 4][None, :]
    v[:, V_SNW:V_SNW + 256] = ssd_nw[j * 256:(j + 1) * 256][None, :]
    v[:, V_GNW:V_GNW + 128] = gla_nw[None, :]
    return v


def m_gw(gate_w, gate_b, j):
    g = np.zeros((17, 64), np.float32)
    g[0:16] = gate_w[:, j * 64:(j + 1) * 64]
    g[16] = gate_b[j * 64:(j + 1) * 64]
    return g


def m_cst():
    c = np.zeros((128, NCST), np.float32)
    i = np.arange(128)
    c[:, K_ID:K_ID + 128] = np.eye(128, dtype=np.float32)
    c[:, K_U:K_U + 128] = (i[:, None] <= i[None, :])
    c[:, K_LM:K_LM + 128] = (i[:, None] > i[None, :])
    ma = np.full((128, 256), NEG, np.float32)
    ma[:, 0:128][i[None, :] >= i[:, None]] = 0.0
    ma[:, 128:256][i[None, :] <= i[:, None]] = 0.0
    mb = ma.copy()
    mb[:, 0:128] = NEG
    c[:, K_MA:K_MA + 256] = ma
    c[:, K_MB:K_MB + 256] = mb
    c[:, K_ONE:K_ONE + 128] = 1.0
    return c


def rope_tables(S):
    inv = (10000.0 ** (-np.arange(64, dtype=np.float32) / np.float32(64))).astype(np.float32)
    ang = (np.arange(S, dtype=np.float32)[:, None] * inv[None, :]).astype(np.float32)
    cos = np.cos(ang).astype(np.float32).T
    sin = np.sin(ang).astype(np.float32).T
    return (np.ascontiguousarray(np.concatenate([cos, cos], 0)),
            np.ascontiguousarray(np.concatenate([-sin, sin], 0)))


_PROGS = {}


def _prog(key, fn):
    if key not in _PROGS:
        _PROGS[key] = fn()[0]
    return _PROGS[key]


def kernel(x, norm1_w, w_in, ssd_conv_w, ssd_conv_b, ssd_dt_bias, ssd_a_log, ssd_d, ssd_norm_w,
           gla_gate_w, gla_gate_b, gla_norm_w, w_branch, w_out, norm2_w, ffn_up, ffn_conv_w,
           ffn_conv_b, ffn_down, final_norm_w):
    f32 = np.float32
    x = np.asarray(x, f32)
    NB, S, _ = x.shape
    depth = w_in.shape[0]
    NQ = 4
    NT = S // NQ
    cores = list(range(NB * NQ))
    xT = [np.ascontiguousarray(x[b].T) for b in range(NB)]
    cosT, sinT = rope_tables(S)
    cst = m_cst()
    ncM = _prog(("M", S), lambda: build_M(S))
    for l in range(depth):
        wl = np.asarray(w_in[l], f32)
        in_maps = []
        for b in range(NB):
            for j in range(NQ):
                in_maps.append({
                    "xT": xT[b], "wall": m_wall(wl, j),
                    "mv": m_mv(norm1_w[l], ssd_conv_w[l], ssd_conv_b[l], ssd_dt_bias[l], ssd_a_log[l], ssd_d[l],
                               ssd_norm_w[l], gla_norm_w[l], j),
                    "gw": m_gw(gla_gate_w[l], gla_gate_b[l], j), "cst": cst, "cosT": cosT, "sinT": sinT})
        res = run_bass_kernel_spmd(ncM, in_maps, core_ids=cores).results
        yT = []
        for b in range(NB):
            y = np.empty((2048, S), ml_dtypes.bfloat16)
            for j in range(NQ):
                ysg = np.asarray(res[b * NQ + j]["ysg"])
                yat = np.asarray(res[b * NQ + j]["yat"])
                y[j * 256:(j + 1) * 256] = ysg[:, 0:256].T
                y[1024 + j * 128:1024 + (j + 1) * 128] = ysg[:, 256:384].T
                y[1536 + j * 128:1536 + (j + 1) * 128] = yat.T
            yT.append(y)
        del res
        final = (l == depth - 1)
        ncT = _prog(("T", NT, final), lambda: build_T(NT, final))
        wts = {"wg": np.ascontiguousarray(wl[:, 9248:12320]), "wb": np.asarray(w_branch[l], f32),
               "wo": np.asarray(w_out[l], f32), "wup": np.asarray(ffn_up[l], f32), "wdn": np.asarray(ffn_down[l], f32),
               "vecs": pack_vecs_T(norm1_w[l], norm2_w[l], final_norm_w, ffn_conv_w[l], ffn_conv_b[l])}
        in_maps = []
        for b in range(NB):
            for q in range(NQ):
                xh = np.zeros((D, NT + 2), f32)
                yh = np.zeros((2048, NT + 2), ml_dtypes.bfloat16)
                lo = q * NT
                xh[:, 2:] = xT[b][:, lo:lo + NT]
                yh[:, 2:] = yT[b][:, lo:lo + NT]
                if q > 0:
                    xh[:, 0:2] = xT[b][:, lo - 2:lo]
                    yh[:, 0:2] = yT[b][:, lo - 2:lo]
                m = {"xT": xh, "yT": yh}
                m.update(wts)
                in_maps.append(m)
        res = run_bass_kernel_spmd(ncT, in_maps, core_ids=cores).results
        xT = [np.ascontiguousarray(np.concatenate([np.asarray(res[b * NQ + q]["oT"]) for q in range(NQ)], axis=1))
              for b in range(NB)]
        del res
    return np.ascontiguousarray(np.stack([xT[b].T for b in range(NB)], axis=0)).astype(f32)
```
